# Optimizing a Trainium2 kernel written in Bass

```python
import jax, jax.numpy as jnp
from jax import lax
import numpy as np

D_MODEL = 1024
BATCH = 2
SEQ = 16384
DEPTH = 4

MEM_LEN = 256
N_HEADS_MLA = 8
QK_NOPE_DIM = 64
QK_ROPE_DIM = 32
QK_HEAD_DIM = QK_NOPE_DIM + QK_ROPE_DIM
V_HEAD_DIM = 64
Q_LORA_RANK = 3 * D_MODEL // 8
KV_LORA_RANK = D_MODEL // 4
MLA_WIDTH = N_HEADS_MLA * V_HEAD_DIM
CONV_WIDTH = D_MODEL // 2
CONV_K = 3
N_HEADS_MEM = 4
MEM_HEAD_DIM = 128
MEM_WIDTH = N_HEADS_MEM * MEM_HEAD_DIM

N_BRANCH = 3
ROPE_BASE = 10000.0
Q_BLOCK = 128
EPS = 1e-6

IN_SIZES = (Q_LORA_RANK, KV_LORA_RANK, QK_ROPE_DIM,
            CONV_WIDTH, CONV_WIDTH, CONV_WIDTH,
            MEM_WIDTH,
            MLA_WIDTH, CONV_WIDTH, MEM_WIDTH,
            N_BRANCH * D_MODEL)
IN_WIDTH = sum(IN_SIZES)

kernel_name = 'hybrid_mla_shortconv_memxattn_encoder'


def rmsnorm(t, g):
    tf = t.astype(jnp.float32)
    tf = tf * lax.rsqrt(jnp.mean(tf * tf, axis=-1, keepdims=True) + EPS)
    return tf.astype(t.dtype) * g


def split_cols(t, sizes):
    return jnp.split(t, np.cumsum(sizes)[:-1].tolist(), axis=-1)


def rope_tables(positions, dtype):
    inv_freq = ROPE_BASE ** (-jnp.arange(0, QK_ROPE_DIM, 2, dtype=jnp.float32) / QK_ROPE_DIM)
    ang = positions.astype(jnp.float32)[..., None] * inv_freq
    return (jnp.cos(ang)[:, :, None, :].astype(dtype),
            jnp.sin(ang)[:, :, None, :].astype(dtype))


def rope_tail(t, cos, sin):
    t_nope, t1, t2 = split_cols(t, (QK_NOPE_DIM, QK_ROPE_DIM // 2, QK_ROPE_DIM // 2))
    return jnp.concatenate([t_nope, t1 * cos - t2 * sin, t2 * cos + t1 * sin], axis=-1)


def blocked_bidirectional_attention(q, k, v):
    B, S, H, Dh = q.shape
    nblk = S // Q_BLOCK
    qb = q.reshape(B, nblk, Q_BLOCK, H, Dh).transpose(1, 0, 2, 3, 4)
    scale = Dh ** -0.5

    def one_block(q_blk):
        s = jnp.einsum('bqhd,bkhd->bhqk', q_blk, k).astype(jnp.float32) * scale
        p = jax.nn.softmax(s, axis=-1).astype(v.dtype)
        return jnp.einsum('bhqk,bkhd->bqhd', p, v)

    out = lax.map(one_block, qb)
    return out.transpose(1, 0, 2, 3, 4).reshape(B, S, H * v.shape[-1])


def centred_short_conv(z, w, b):
    out = lax.conv_general_dilated(
        z, w[:, None, :], window_strides=(1,),
        padding=((CONV_K // 2, CONV_K // 2),),
        dimension_numbers=('NWC', 'WIO', 'NWC'),
        feature_group_count=z.shape[-1])
    return out + b


def hybrid_layer(x, mem, cos, sin, norm_g, w_in, b_gate, q_norm_g, w_uq, kv_norm_g, w_ukv,
                 q_head_g, k_head_g, conv_w, conv_b, mem_norm_g, w_mkv, mem_q_g, mem_k_g,
                 w_br_attn, w_br_conv, w_br_mem, w_out):
    B, S, _ = x.shape
    M = mem.shape[1]
    h = rmsnorm(x, norm_g)
    proj = h @ w_in
    (q_lat, kv_lat, k_pe, c_b, c_c, c_u, q_mem,
     g_attn, g_conv, g_mem, r) = split_cols(proj, IN_SIZES)

    q = (rmsnorm(q_lat, q_norm_g) @ w_uq).reshape(B, S, N_HEADS_MLA, QK_HEAD_DIM)
    kv = (rmsnorm(kv_lat, kv_norm_g) @ w_ukv).reshape(B, S, N_HEADS_MLA, QK_NOPE_DIM + V_HEAD_DIM)
    k_nope, v = split_cols(kv, (QK_NOPE_DIM, V_HEAD_DIM))
    k_rope = jnp.broadcast_to(k_pe[:, :, None, :], (B, S, N_HEADS_MLA, QK_ROPE_DIM))
    k = jnp.concatenate([k_nope, k_rope], axis=-1)
    q = rope_tail(rmsnorm(q, q_head_g), cos, sin)
    k = rope_tail(rmsnorm(k, k_head_g), cos, sin)
    o_attn = blocked_bidirectional_attention(q, k, v) * jax.nn.silu(g_attn)

    o_conv = c_b * centred_short_conv(c_c * c_u, conv_w, conv_b) * jax.nn.silu(g_conv)

    mkv = (rmsnorm(mem, mem_norm_g) @ w_mkv).reshape(B, M, N_HEADS_MEM, 2 * MEM_HEAD_DIM)
    m_k, m_v = split_cols(mkv, (MEM_HEAD_DIM, MEM_HEAD_DIM))
    mq = rmsnorm(q_mem.reshape(B, S, N_HEADS_MEM, MEM_HEAD_DIM), mem_q_g)
    m_k = rmsnorm(m_k, mem_k_g)
    s = jnp.einsum('bshd,bmhd->bhsm', mq, m_k).astype(jnp.float32) * (MEM_HEAD_DIM ** -0.5)
    p = jax.nn.softmax(s, axis=-1).astype(m_v.dtype)
    o_mem = jnp.einsum('bhsm,bmhd->bshd', p, m_v).reshape(B, S, MEM_WIDTH) * jax.nn.silu(g_mem)

    r_attn, r_conv, r_mem = split_cols(jax.nn.sigmoid(r + b_gate), (D_MODEL, D_MODEL, D_MODEL))
    y = r_attn * (o_attn @ w_br_attn) + r_conv * (o_conv @ w_br_conv) + r_mem * (o_mem @ w_br_mem)
    return x + y @ w_out


def setup_inputs(seed: int = 0) -> dict:
    key = jax.random.key(seed)
    ks = jax.random.split(key, 24)

    def nrm(k, shape, scale):
        return jax.random.normal(k, shape, jnp.float32) * scale

    def gain(k, shape):
        return 1.0 + 0.1 * jax.random.normal(k, shape, jnp.float32)

    x = nrm(ks[0], (BATCH, SEQ, D_MODEL), 1.0)
    mem = nrm(ks[1], (BATCH, MEM_LEN, D_MODEL), 1.0)
    offset = jax.random.randint(ks[2], (BATCH, 1), 0, 1024, dtype=jnp.int32)
    positions = jnp.arange(SEQ, dtype=jnp.int32)[None, :] + offset
    return {
        'x': x,
        'mem': mem,
        'positions': positions,
        'norm_g': gain(ks[3], (DEPTH, D_MODEL)),
        'w_in': nrm(ks[4], (DEPTH, D_MODEL, IN_WIDTH), D_MODEL ** -0.5),
        'b_gate': nrm(ks[5], (DEPTH, N_BRANCH * D_MODEL), 0.1),
        'q_norm_g': gain(ks[6], (DEPTH, Q_LORA_RANK)),
        'w_uq': nrm(ks[7], (DEPTH, Q_LORA_RANK, N_HEADS_MLA * QK_HEAD_DIM), Q_LORA_RANK ** -0.5),
        'kv_norm_g': gain(ks[8], (DEPTH, KV_LORA_RANK)),
        'w_ukv': nrm(ks[9], (DEPTH, KV_LORA_RANK, N_HEADS_MLA * (QK_NOPE_DIM + V_HEAD_DIM)), KV_LORA_RANK ** -0.5),
        'q_head_g': gain(ks[10], (DEPTH, QK_HEAD_DIM)),
        'k_head_g': gain(ks[11], (DEPTH, QK_HEAD_DIM)),
        'conv_w': nrm(ks[12], (DEPTH, CONV_K, CONV_WIDTH), CONV_K ** -0.5),
        'conv_b': nrm(ks[13], (DEPTH, CONV_WIDTH), 0.1),
        'mem_norm_g': gain(ks[14], (DEPTH, D_MODEL)),
        'w_mkv': nrm(ks[15], (DEPTH, D_MODEL, 2 * MEM_WIDTH), D_MODEL ** -0.5),
        'mem_q_g': gain(ks[16], (DEPTH, MEM_HEAD_DIM)),
        'mem_k_g': gain(ks[17], (DEPTH, MEM_HEAD_DIM)),
        'w_br_attn': nrm(ks[18], (DEPTH, MLA_WIDTH, D_MODEL), MLA_WIDTH ** -0.5),
        'w_br_conv': nrm(ks[19], (DEPTH, CONV_WIDTH, D_MODEL), CONV_WIDTH ** -0.5),
        'w_br_mem': nrm(ks[20], (DEPTH, MEM_WIDTH, D_MODEL), MEM_WIDTH ** -0.5),
        'w_out': nrm(ks[21], (DEPTH, D_MODEL, D_MODEL), D_MODEL ** -0.5),
    }


def reference(x, mem, positions, norm_g, w_in, b_gate, q_norm_g, w_uq, kv_norm_g, w_ukv,
              q_head_g, k_head_g, conv_w, conv_b, mem_norm_g, w_mkv, mem_q_g, mem_k_g,
              w_br_attn, w_br_conv, w_br_mem, w_out):
    cos, sin = rope_tables(positions, x.dtype)
    for i in range(DEPTH):
        x = hybrid_layer(x, mem, cos, sin, norm_g[i], w_in[i], b_gate[i], q_norm_g[i], w_uq[i],
                         kv_norm_g[i], w_ukv[i], q_head_g[i], k_head_g[i], conv_w[i], conv_b[i],
                         mem_norm_g[i], w_mkv[i], mem_q_g[i], mem_k_g[i],
                         w_br_attn[i], w_br_conv[i], w_br_mem[i], w_out[i])
    return x
```

```python
import contextlib
import numpy as np
import ml_dtypes
import jax
import jax.numpy as jnp
import concourse.bass as bass
import concourse.mybir as mybir
from concourse.bass_utils import run_bass_kernel_spmd

F32 = mybir.dt.float32
BF16 = mybir.dt.bfloat16
I32 = mybir.dt.int32
AF = mybir.ActivationFunctionType
ALU = mybir.AluOpType

D = 1024
NH = 8
DH = 96
DV = 64
QL = 384
KVL = 256
MEM = 256
EPS = 1e-6
INW = 7328
O_QL, O_KVL, O_KPE, O_CB, O_CC, O_CU, O_QM, O_GA, O_GC, O_GM, O_RA, O_RC, O_RM = (
    0, 384, 640, 672, 1184, 1696, 2208, 2720, 3232, 3744, 4256, 5280, 6304)
NPV = 72
TWO_PI = 6.283185307179586
C1 = 6.28125
C2 = TWO_PI - C1
MAGIC = 12582912.0
PI_SAFE = 3.1415925


class Buf:
    __slots__ = ("w", "r")

    def __init__(self):
        self.w = None
        self.r = []


class Tile:
    def __init__(self, t):
        self.t = t
        self.b = Buf()

    def __getitem__(self, idx):
        return self.t[idx]


class Ring:
    def __init__(self, tiles):
        self.tiles = tiles
        self.i = 0

    def next(self):
        t = self.tiles[self.i % len(self.tiles)]
        self.i += 1
        return t


NDMA = 12


class Sched:
    ENGS = ("pe", "act", "dve", "pool", "sp")

    def __init__(self, nc, stack):
        self.nc = nc
        self.sem = {}
        for e in ("pe", "act", "dve", "pool"):
            self.sem[e] = stack.enter_context(nc.semaphore("s_" + e))
        for i in range(NDMA):
            self.sem["d%d" % i] = stack.enter_context(nc.semaphore("s_d%d" % i))
        self.cnt = {k: 0 for k in self.sem}
        self.ops = {e: [] for e in self.ENGS}
        self.seen = {e: {} for e in self.ENGS}
        self.ndma = 0

    def _deps(self, eng, reads, writes):
        deps = {}
        for t in reads:
            b = t.b
            if b.w is not None:
                k, c = b.w
                deps[k] = max(deps.get(k, 0), c)
        for t in writes:
            b = t.b
            if b.w is not None:
                k, c = b.w
                deps[k] = max(deps.get(k, 0), c)
            for (k, c) in b.r:
                deps[k] = max(deps.get(k, 0), c)
        return deps

    def op(self, eng, fn, r=(), w=()):
        deps = self._deps(eng, r, w)
        if eng == "sp":
            key = "d%d" % (self.ndma % NDMA)
            self.ndma += 1
            if self.cnt[key] > 0:
                deps[key] = max(deps.get(key, 0), self.cnt[key])
            inc = 16
        else:
            key = eng
            inc = 1
        waits = []
        seen = self.seen[eng]
        for k, c in deps.items():
            if k == "pe" and eng == "pe":
                continue
            if seen.get(k, 0) < c:
                seen[k] = c
                waits.append((self.sem[k], c))
        self.cnt[key] += inc
        me = (key, self.cnt[key])
        self.ops[eng].append((waits, fn, self.sem[key], inc))
        for t in r:
            t.b.r.append(me)
        for t in w:
            t.b.w = me
            t.b.r = []

    def _mk(self, eng, name, r, w, kw):
        self.op(eng, (lambda e: getattr(e, name)(**kw)), r, w)

    def pe(self, name, r=(), w=(), **kw):
        self._mk("pe", name, r, w, kw)

    def act(self, name, r=(), w=(), **kw):
        self._mk("act", name, r, w, kw)

    def dve(self, name, r=(), w=(), **kw):
        self._mk("dve", name, r, w, kw)

    def pool(self, name, r=(), w=(), **kw):
        self._mk("pool", name, r, w, kw)

    def dma(self, r=(), w=(), **kw):
        self._mk("sp", "dma_start", r, w, kw)

    def emit(self, final_tiles):
        nc = self.nc
        fin = {k: c for k, c in self.cnt.items() if k.startswith("d") and c > 0}
        ops = self.ops
        sem = self.sem

        def run(e, lst, extra=None):
            for waits, fn, s, inc in lst:
                for (ws, wv) in waits:
                    e.wait_ge(ws, wv)
                fn(e).then_inc(s, inc)
            if extra:
                for k, c in extra.items():
                    e.wait_ge(sem[k], c)

        with nc.Block() as block:
            @block.sync
            def _(e):
                run(e, ops["sp"], fin)

            @block.tensor
            def _(e):
                run(e, ops["pe"])

            @block.scalar
            def _(e):
                run(e, ops["act"])

            @block.vector
            def _(e):
                run(e, ops["dve"])

            @block.gpsimd
            def _(e):
                run(e, ops["pool"])


class Ctx:
    def __init__(self, nc, stack):
        self.nc = nc
        self.stack = stack
        self.n = 0

    def sb(self, shape, dt, name=None):
        self.n += 1
        nb = int(np.prod(shape[1:])) * (2 if dt == BF16 else 4)
        self.bytes = getattr(self, "bytes", 0) + nb
        t = self.stack.enter_context(self.nc.sbuf_tensor("%s_%d" % (name or "sb", self.n), list(shape), dt))
        return Tile(t)

    def ps(self, shape, dt=F32, name=None):
        self.n += 1
        t = self.stack.enter_context(self.nc.psum_tensor("%s_%d" % (name or "ps", self.n), list(shape), dt))
        return Tile(t)

    def sb_ring(self, n, shape, dt, name=None):
        return Ring([self.sb(shape, dt, name) for _ in range(n)])

    def ps_ring(self, n, shape, dt=F32, name=None):
        return Ring([self.ps(shape, dt, name) for _ in range(n)])

    def dram_in(self, name, shape, dt):
        return Tile(self.nc.dram_tensor(name, list(shape), dt, kind="ExternalInput").ap())

    def dram_out(self, name, shape, dt):
        return Tile(self.nc.dram_tensor(name, list(shape), dt, kind="ExternalOutput").ap())

    def dram_tmp(self, name, shape, dt):
        return Tile(self.nc.dram_tensor(name, list(shape), dt).ap())


def load_weight_bf16(S, C, w_dram, r0, nk, c0, ncols, dst, dst_c0, stage_ring, cast_i=[0]):
    step = 512
    for k in range(nk):
        for cc in range(0, ncols, step):
            n = min(step, ncols - cc)
            st = stage_ring.next()
            S.dma(out=st[:, 0:n], in_=w_dram[r0 + k * 128:r0 + (k + 1) * 128, c0 + cc:c0 + cc + n],
                  r=[w_dram], w=[st])
            which = cast_i[0] % 2
            cast_i[0] += 1
            if which == 0:
                S.pool("tensor_copy", out=dst[:, k, dst_c0 + cc:dst_c0 + cc + n], in_=st[:, 0:n], r=[st], w=[dst])
            else:
                S.dve("tensor_copy", out=dst[:, k, dst_c0 + cc:dst_c0 + cc + n], in_=st[:, 0:n], r=[st], w=[dst])


def build_phase_a(T):
    ST = min(T, 512)
    NST = T // ST
    NB = ST // 128 + 1
    STE = NB * 128
    NT = ST // 512
    TE = T + 128

    nc = bass.Bass("TRN2", target_bir_lowering=False)
    with contextlib.ExitStack() as stack:
        C = Ctx(nc, stack)
        S = Sched(nc, stack)
        xe = C.dram_in("xe", [TE, D], F32)
        memx = C.dram_in("memx", [MEM, D], F32)
        posr = C.dram_in("posr", [96, T], I32)
        pvec_d = C.dram_in("pvec", [128, NPV], F32)
        ident_d = C.dram_in("ident", [128, 128], F32)
        w_in = C.dram_in("w_in", [D, INW], F32)
        w_uq = C.dram_in("w_uq", [QL, NH * DH], F32)
        w_uqr = C.dram_in("w_uqr", [QL, NH * DH], F32)
        w_ukv = C.dram_in("w_ukv", [KVL, 1024], F32)
        w_kpe = C.dram_in("w_kpe", [D, 192], F32)
        w_mkv = C.dram_in("w_mkv", [D, 1024], F32)
        w_bc = C.dram_in("w_bc", [512, D], F32)
        w_bm = C.dram_in("w_bm", [512, D], F32)
        qt_o = C.dram_out("qt", [NH * DH, T], BF16)
        kt_o = C.dram_out("kt", [NH * DH, T], BF16)
        v_o = C.dram_out("v", [T, NH * DV], BF16)
        yp_o = C.dram_out("ypc", [D, T], F32)
        ypm_o = C.dram_out("ypm", [D, T], F32)
        ga_o = C.dram_out("ga", [512, T], BF16)
        ra_o = C.dram_out("ra", [D, T], F32)
        outs = [qt_o, kt_o, v_o, yp_o, ypm_o, ga_o, ra_o]

        pv = C.sb([128, NPV], F32, "pv")
        ident_f = C.sb([128, 128], F32, "identf")
        ident = C.sb([128, 128], BF16, "ident")
        ones = C.sb([128, 128], BF16, "ones")
        hT = C.sb([128, 8, STE], BF16, "hT")
        zT = C.sb([128, 4, ST + 2], BF16, "zT")
        ctab = C.sb([96, ST], F32, "ctab")
        stab = C.sb([96, ST], F32, "stab")
        wg = [C.sb([128, 8, 1024], BF16, "wg") for _ in range(2)]
        wuq_b = C.sb([128, 3, NH * DH], BF16, "wuq")
        wuqr_b = C.sb([128, 3, NH * DH], BF16, "wuqr")
        wukv_b = C.sb([128, 2, 1024], BF16, "wukv")
        wkpe_b = C.sb([128, 8, 192], BF16, "wkpe")
        wbc_b = C.sb([128, 4, D], BF16, "wbc")
        wbm_b = C.sb([128, 4, D], BF16, "wbm")
        mkT = C.sb([128, 4, MEM], BF16, "mkT")
        mv_b = C.sb([128, 2, 512], BF16, "mv")
        stage = C.sb_ring(3, [128, 512], F32, "stage")
        posi = C.sb_ring(2, [96, 512], I32, "posi")
        f32r = C.sb_ring(8, [128, 512], F32, "f32r")
        bf16r = C.sb_ring(8, [128, 512], BF16, "bf16r")
        xr = C.sb_ring(2, [128, D], F32, "xr")
        xbr = C.sb_ring(2, [128, D], BF16, "xbr")
        sqfr = C.sb_ring(1, [128, D], F32, "sqfr")
        col = C.sb_ring(6, [128, 1], F32, "col")
        ql_b = C.sb([128, 3, 512], BF16, "ql")
        kvl_b = C.sb([128, 2, 512], BF16, "kvl")
        sqq_b = C.sb([128, 3, 512], BF16, "sqq")
        sqkv_b = C.sb([128, 2, 512], BF16, "sqkv")
        oc_b = C.sb([128, 4, 512], BF16, "oc")
        om_b = C.sb([128, 4, 512], BF16, "om")
        sqks = [C.sb([96, 512], BF16, "sqk") for _ in range(2)]
        epsA = C.sb([128, 512], F32, "epsA")
        ckv = C.sb([128, 512], F32, "ckv")
        kpr = C.sb([128, 512], F32, "kpr")
        P = C.ps_ring(7, [128, 512], F32, "P")
        PT = C.ps([128, 1024], BF16, "PT")

        S.dma(out=pv[:], in_=pvec_d[:, :], r=[pvec_d], w=[pv])
        S.dma(out=ident_f[:], in_=ident_d[:, :], r=[ident_d], w=[ident_f])
        S.dve("tensor_copy", out=ident[:], in_=ident_f[:], r=[ident_f], w=[ident])
        S.pool("memset", ap=ones[:], constant=1.0, w=[ones])

        load_weight_bf16(S, C, w_uq, 0, 3, 0, NH * DH, wuq_b, 0, stage)
        load_weight_bf16(S, C, w_uqr, 0, 3, 0, NH * DH, wuqr_b, 0, stage)
        load_weight_bf16(S, C, w_ukv, 0, 2, 0, 1024, wukv_b, 0, stage)
        load_weight_bf16(S, C, w_kpe, 0, 8, 0, 192, wkpe_b, 0, stage)
        load_weight_bf16(S, C, w_bc, 0, 4, 0, D, wbc_b, 0, stage)
        load_weight_bf16(S, C, w_bm, 0, 4, 0, D, wbm_b, 0, stage)

        def rstd_from(ps_ap_fn, nrows, scale, bias_tile, out_tile):
            pass

        def rmsnorm_T(src_dram, row0, nblk, dstT, gcol0):
            for blk in range(nblk):
                xt = xr.next()
                S.dma(
                    out=xt[:], in_=src_dram[row0 + blk * 128:row0 + (blk + 1) * 128, :],
                    r=[src_dram], w=[xt])
                ss = col.next()
                xb = xbr.next()
                sqf = sqfr.next()
                S.act("activation", out=sqf[:], in_=xt[:], func=AF.Square, r=[xt], w=[sqf])
                S.dve("reduce_sum", out=ss[:], in_=sqf[:], axis=mybir.AxisListType.X, r=[sqf], w=[ss])
                S.dve("tensor_scalar", out=ss[:], in0=ss[:], scalar1=1.0 / D, scalar2=EPS,
                                                       op0=ALU.mult, op1=ALU.add, r=[ss], w=[ss])
                S.act("activation", out=ss[:], in_=ss[:], func=AF.Ln, r=[ss], w=[ss])
                S.act("activation", out=ss[:], in_=ss[:], func=AF.Exp, scale=-0.5, r=[ss], w=[ss])
                S.dve("tensor_scalar",
                    out=xb[:], in0=xt[:], scalar1=ss[:, 0:1], scalar2=None, op0=ALU.mult, r=[xt, ss], w=[xb])
                for k in range(8):
                    S.pe("transpose",
                        out=PT[:, k * 128:(k + 1) * 128], in_=xb[:, k * 128:(k + 1) * 128], identity=ident[:],
                        r=[xb, ident], w=[PT])
                for k in range(8):
                    S.dve("tensor_scalar", out=dstT[:, k, blk * 128:(blk + 1) * 128],
                          in0=PT[:, k * 128:(k + 1) * 128], scalar1=pv[:, gcol0 + k:gcol0 + k + 1], scalar2=None,
                          op0=ALU.mult, r=[PT, pv], w=[dstT])

        def ln_exp_rstd(src, dst, rows, scale, bias):
            S.dve("tensor_scalar", out=dst[0:rows, :], in0=src[0:rows, :], scalar1=scale, scalar2=bias,
                                            op0=ALU.mult, op1=ALU.add, r=[src], w=[dst])
            S.act("activation", out=dst[0:rows, :], in_=dst[0:rows, :], func=AF.Ln, r=[dst], w=[dst])
            S.act("activation", out=dst[0:rows, :], in_=dst[0:rows, :], func=AF.Exp, scale=-0.5,
                  r=[dst], w=[dst])

        hmT = hT
        rmsnorm_T(memx, 0, 2, hmT, 59)
        load_weight_bf16(S, C, w_mkv, 0, 8, 0, 1024, wg[0], 0, stage)
        for h in range(4):
            p = P.next()
            for k in range(8):
                S.pe("matmul", out=p[:, 0:MEM], lhsT=wg[0][:, k, h * 128:(h + 1) * 128],
                                                       rhs=hmT[:, k, 0:MEM], start=(k == 0), stop=(k == 7),
                     r=[wg[0], hmT], w=[p])
            sq = bf16r.next()
            S.act("activation", out=sq[:, 0:MEM], in_=p[:, 0:MEM], func=AF.Square, r=[p], w=[sq])
            p2 = P.next()
            S.pe("matmul", out=p2[:, 0:MEM], lhsT=ones[:], rhs=sq[:, 0:MEM], start=True, stop=True,
                 r=[ones, sq], w=[p2])
            rs = f32r.next()
            S.dve("tensor_scalar", out=rs[:, 0:MEM], in0=p2[:, 0:MEM], scalar1=1.0 / 128,
                                                          scalar2=EPS, op0=ALU.mult, op1=ALU.add, r=[p2], w=[rs])
            S.act("activation", out=rs[:, 0:MEM], in_=rs[:, 0:MEM], func=AF.Ln, r=[rs], w=[rs])
            S.act("activation", out=rs[:, 0:MEM], in_=rs[:, 0:MEM], func=AF.Exp, scale=-0.5,
                  r=[rs], w=[rs])
            S.dve("scalar_tensor_tensor",
                out=mkT[:, h, :], in0=p[:, 0:MEM], scalar=pv[:, 58:59], in1=rs[:, 0:MEM], op0=ALU.mult, op1=ALU.mult,
                r=[p, rs, pv], w=[mkT])
        for blk in range(2):
            p = P.next()
            for k in range(8):
                S.pe("matmul", out=p[:], lhsT=hmT[:, k, blk * 128:(blk + 1) * 128],
                                                           rhs=wg[0][:, k, 512:1024], start=(k == 0), stop=(k == 7),
                     r=[wg[0], hmT], w=[p])
            S.act("copy", out=mv_b[:, blk, :], in_=p[:], r=[p], w=[mv_b])

        def proj(wt, c0, M, tok0, N, p):
            for k in range(8):
                S.pe("matmul", out=p[0:M, 0:N], lhsT=wt[:, k, c0:c0 + M], rhs=hT[:, k, tok0:tok0 + N],
                                             start=(k == 0), stop=(k == 7), r=[wt, hT], w=[p])

        gsel = [0]

        def load_group(c0, ncols):
            g = wg[gsel[0] % 2]
            gsel[0] += 1
            load_weight_bf16(S, C, w_in, 0, 8, c0, ncols, g, 0, stage)
            return g

        for st in range(NST):
            t_base = st * ST
            for c0 in range(0, ST, 512):
                pi_t = posi.next()
                S.dma(out=pi_t[:], in_=posr[:, t_base + c0:t_base + c0 + 512], r=[posr], w=[pi_t])
                ang = f32r.next()
                S.dve("tensor_copy", out=ang[0:96, :], in_=pi_t[:], r=[pi_t], w=[ang])
                S.dve("tensor_scalar", out=ang[0:96, :], in0=ang[0:96, :], scalar1=pv[0:96, 67:68],
                                                         scalar2=None, op0=ALU.mult, r=[ang, pv], w=[ang])
                for (dst, shift) in ((stab, 0.0), (ctab, TWO_PI / 4)):
                    a2 = f32r.next()
                    kk = f32r.next()
                    S.dve("tensor_scalar",
                        out=a2[0:96, :], in0=ang[0:96, :], scalar1=shift, scalar2=None, op0=ALU.add,
                        r=[ang], w=[a2])
                    S.dve("tensor_scalar",
                        out=kk[0:96, :], in0=a2[0:96, :], scalar1=1.0 / TWO_PI, scalar2=MAGIC, op0=ALU.mult, op1=ALU.add,
                        r=[a2], w=[kk])
                    S.dve("tensor_scalar",
                        out=kk[0:96, :], in0=kk[0:96, :], scalar1=-MAGIC, scalar2=None, op0=ALU.add, r=[kk], w=[kk])
                    S.dve("scalar_tensor_tensor",
                        out=a2[0:96, :], in0=kk[0:96, :], scalar=-C1, in1=a2[0:96, :], op0=ALU.mult, op1=ALU.add,
                        r=[a2, kk], w=[a2])
                    S.dve("scalar_tensor_tensor",
                        out=a2[0:96, :], in0=kk[0:96, :], scalar=-C2, in1=a2[0:96, :], op0=ALU.mult, op1=ALU.add,
                        r=[a2, kk], w=[a2])
                    S.dve("tensor_scalar",
                        out=a2[0:96, :], in0=a2[0:96, :], scalar1=-PI_SAFE, scalar2=PI_SAFE, op0=ALU.max, op1=ALU.min,
                        r=[a2], w=[a2])
                    S.act("activation",
                        out=dst[0:96, c0:c0 + 512], in_=a2[0:96, :], func=AF.Sin, r=[a2], w=[dst])
                S.dve("tensor_scalar", out=stab[0:96, c0:c0 + 512], in0=stab[0:96, c0:c0 + 512],
                                                       scalar1=pv[0:96, 68:69], scalar2=None, op0=ALU.mult,
                      r=[stab, pv], w=[stab])


            rmsnorm_T(xe, t_base, NB, hT, 0)
            g1 = load_group(0, 640)
            g2 = load_group(O_CC, 1024)
            for ti in range(NT):
                tok0 = 1 + ti * 512
                tg = t_base + ti * 512
                for m in range(3):
                    p = P.next()
                    proj(g1, m * 128, 128, tok0, 512, p)
                    S.act("activation", out=sqq_b[:, m, :], in_=p[:], func=AF.Square,
                          r=[p], w=[sqq_b])
                    S.dve("tensor_scalar", out=ql_b[:, m, :], in0=p[:], scalar1=pv[:, 8 + m:9 + m],
                                                              scalar2=None, op0=ALU.mult, r=[p, pv], w=[ql_b])
                pq = P.next()
                for m in range(3):
                    S.pe("matmul", out=pq[:], lhsT=ones[:], rhs=sqq_b[:, m, :], start=(m == 0),
                                                        stop=(m == 2), r=[ones, sqq_b], w=[pq])
                S.dve("tensor_scalar",
                    out=epsA[:], in0=pq[:], scalar1=EPS / QL, scalar2=EPS * EPS, op0=ALU.mult, op1=ALU.add,
                    r=[pq], w=[epsA])
                for m in range(2):
                    p = P.next()
                    proj(g1, QL + m * 128, 128, tok0, 512, p)
                    S.act("activation", out=sqkv_b[:, m, :], in_=p[:], func=AF.Square,
                          r=[p], w=[sqkv_b])
                    S.dve("tensor_scalar", out=kvl_b[:, m, :], in0=p[:],
                                                              scalar1=pv[:, 11 + m:12 + m], scalar2=None, op0=ALU.mult,
                          r=[p, pv], w=[kvl_b])
                pk = P.next()
                for m in range(2):
                    S.pe("matmul", out=pk[:], lhsT=ones[:], rhs=sqkv_b[:, m, :], start=(m == 0),
                                                        stop=(m == 1), r=[ones, sqkv_b], w=[pk])
                ln_exp_rstd(pk, ckv, 128, 1.0 / KVL, EPS)
                for j in range(4):
                    pc = P.next()
                    for m in range(2):
                        S.pe("matmul", out=
                            pc[:, 0:1], lhsT=sqkv_b[:, m, j * 128:(j + 1) * 128], rhs=ones[:, 0:1],
                            start=(m == 0), stop=(m == 1), r=[ones, sqkv_b], w=[pc])
                    cv = col.next()
                    S.dve("tensor_scalar", out=cv[:], in0=pc[:, 0:1], scalar1=1.0 / KVL,
                                                                  scalar2=EPS, op0=ALU.mult, op1=ALU.add,
                          r=[pc], w=[cv])
                    S.act("activation", out=cv[:], in_=cv[:], func=AF.Ln, r=[cv], w=[cv])
                    S.act("activation", out=cv[:], in_=cv[:], func=AF.Exp, scale=-0.5,
                          r=[cv], w=[cv])
                    pvv = P.next()
                    for m in range(2):
                        S.pe("matmul", out=
                            pvv[:], lhsT=kvl_b[:, m, j * 128:(j + 1) * 128], rhs=wukv_b[:, m, 512:1024],
                            start=(m == 0), stop=(m == 1), r=[kvl_b, wukv_b], w=[pvv])
                    vb = bf16r.next()
                    S.dve("tensor_scalar",
                        out=vb[:], in0=pvv[:], scalar1=cv[:, 0:1], scalar2=None, op0=ALU.mult,
                        r=[pvv, cv], w=[vb])
                    S.dma(
                        out=v_o[tg + j * 128:tg + (j + 1) * 128, :], in_=vb[:], r=[vb], w=[v_o])
                pkp = P.next()
                proj(wkpe_b, 0, 96, tok0, 512, pkp)
                pkr = P.next()
                proj(wkpe_b, 96, 96, tok0, 512, pkr)
                for sqk in sqks:
                    S.act("activation", out=sqk[64:96, :], in_=pkp[64:96, :], func=AF.Square, r=[pkp], w=[sqk])
                t2 = f32r.next()
                S.dve("scalar_tensor_tensor",
                    out=kpr[64:96, :], in0=pkp[64:96, :], scalar=pv[64:96, 15:16], in1=ctab[64:96, ti * 512:ti * 512 + 512],
                    op0=ALU.mult, op1=ALU.mult, r=[pkp, pv, ctab], w=[kpr])
                S.dve("scalar_tensor_tensor",
                    out=t2[64:96, :], in0=pkr[64:96, :], scalar=pv[64:96, 16:17], in1=stab[64:96, ti * 512:ti * 512 + 512],
                    op0=ALU.mult, op1=ALU.mult, r=[pkr, pv, stab], w=[t2])
                S.pool("tensor_tensor", out=kpr[64:96, :], in0=kpr[64:96, :],
                                                                in1=t2[64:96, :], op=ALU.add, r=[kpr, t2], w=[kpr])
                for h in range(NH):
                    pu = P.next()
                    for m in range(2):
                        S.pe("matmul", out=
                            pu[0:64, :], lhsT=wukv_b[:, m, h * 64:(h + 1) * 64], rhs=kvl_b[:, m, :],
                            start=(m == 0), stop=(m == 1), r=[wukv_b, kvl_b], w=[pu])
                    tmp = f32r.next()
                    S.dve("tensor_tensor",
                        out=tmp[0:64, :], in0=pu[0:64, :], in1=ckv[0:64, :], op=ALU.mult, r=[pu, ckv], w=[tmp])
                    sqh = sqks[h % 2]
                    S.act("activation", out=sqh[0:64, :], in_=tmp[0:64, :], func=AF.Square, r=[tmp], w=[sqh])
                    pss = P.next()
                    S.pe("matmul", out=pss[0:96, :], lhsT=ones[0:96, 0:96], rhs=sqh[0:96, :], start=True, stop=True,
                         r=[ones, sqh], w=[pss])
                    rs = f32r.next()
                    ln_exp_rstd(pss, rs, 96, 1.0 / DH, EPS)
                    kb = bf16r.next()
                    S.dve("scalar_tensor_tensor",
                        out=kb[0:64, :], in0=tmp[0:64, :], scalar=pv[0:64, 15:16], in1=rs[0:64, :],
                        op0=ALU.mult, op1=ALU.mult, r=[tmp, rs, pv], w=[kb])
                    S.pool("tensor_tensor",
                        out=kb[64:96, :], in0=kpr[64:96, :], in1=rs[64:96, :], op=ALU.mult, r=[kpr, rs], w=[kb])
                    S.dma(
                        out=kt_o[h * DH:(h + 1) * DH, tg:tg + 512], in_=kb[0:96, :], r=[kb], w=[kt_o])
                for h in range(NH):
                    pu = P.next()
                    pr_ = P.next()
                    for m in range(3):
                        S.pe("matmul", out=
                            pu[0:96, :], lhsT=wuq_b[:, m, h * DH:(h + 1) * DH], rhs=ql_b[:, m, :],
                            start=(m == 0), stop=(m == 2), r=[wuq_b, ql_b], w=[pu])
                    for m in range(3):
                        S.pe("matmul", out=
                            pr_[0:96, :], lhsT=wuqr_b[:, m, h * DH:(h + 1) * DH], rhs=ql_b[:, m, :],
                            start=(m == 0), stop=(m == 2), r=[wuqr_b, ql_b], w=[pr_])
                    sqh = bf16r.next()
                    S.act("activation", out=sqh[0:96, :], in_=pu[0:96, :], func=AF.Square,
                          r=[pu], w=[sqh])
                    pss = P.next()
                    S.pe("matmul", out=pss[0:96, :], lhsT=ones[0:96, 0:96], rhs=sqh[0:96, :],
                                                              start=True, stop=True, r=[ones, sqh], w=[pss])
                    rs = f32r.next()
                    S.dve("scalar_tensor_tensor",
                        out=rs[0:96, :], in0=pss[0:96, :], scalar=1.0 / DH, in1=epsA[0:96, :], op0=ALU.mult,
                        op1=ALU.add, r=[pss, epsA], w=[rs])
                    S.act("activation", out=rs[0:96, :], in_=rs[0:96, :], func=AF.Ln, r=[rs], w=[rs])
                    S.act("activation", out=rs[0:96, :], in_=rs[0:96, :], func=AF.Exp, scale=-0.5,
                          r=[rs], w=[rs])
                    qn = f32r.next()
                    qr = f32r.next()
                    S.dve("scalar_tensor_tensor",
                        out=qn[0:96, :], in0=pu[0:96, :], scalar=pv[0:96, 13:14], in1=rs[0:96, :],
                        op0=ALU.mult, op1=ALU.mult, r=[pu, rs, pv], w=[qn])
                    S.dve("scalar_tensor_tensor",
                        out=qr[0:96, :], in0=pr_[0:96, :], scalar=pv[0:96, 14:15], in1=rs[0:96, :],
                        op0=ALU.mult, op1=ALU.mult, r=[pr_, rs, pv], w=[qr])
                    S.pool("tensor_tensor",
                        out=qn[0:96, :], in0=qn[0:96, :], in1=ctab[0:96, ti * 512:ti * 512 + 512], op=ALU.mult,
                        r=[qn, ctab], w=[qn])
                    S.pool("tensor_tensor",
                        out=qr[0:96, :], in0=qr[0:96, :], in1=stab[0:96, ti * 512:ti * 512 + 512], op=ALU.mult,
                        r=[qr, stab], w=[qr])
                    qb = bf16r.next()
                    S.dve("tensor_tensor",
                        out=qb[0:96, :], in0=qn[0:96, :], in1=qr[0:96, :], op=ALU.add, r=[qn, qr], w=[qb])
                    S.dma(
                        out=qt_o[h * DH:(h + 1) * DH, tg:tg + 512], in_=qb[0:96, :], r=[qb], w=[qt_o])

            g3 = load_group(O_CB, 512)
            ranges = [(i * 512, 512) for i in range(NT)] + [(ST, 2)]
            for (j0, n) in ranges:
                for m in range(4):
                    pc = P.next()
                    pu = P.next()
                    proj(g2, m * 128, 128, j0, n, pc)
                    proj(g2, 512 + m * 128, 128, j0, n, pu)
                    cs = f32r.next()
                    S.act("copy", out=cs[:, 0:n], in_=pc[:, 0:n], r=[pc], w=[cs])
                    S.dve("tensor_tensor",
                        out=zT[:, m, j0:j0 + n], in0=cs[:, 0:n], in1=pu[:, 0:n], op=ALU.mult,
                        r=[cs, pu], w=[zT])
            load_weight_bf16(S, C, w_in, 0, 8, O_GC, 512, g3, 512, stage)
            g4 = load_group(O_RC, 1024)
            for ti in range(NT):
                tok0 = 1 + ti * 512
                tg = t_base + ti * 512
                for m in range(4):
                    pb = P.next()
                    pg = P.next()
                    proj(g3, m * 128, 128, tok0, 512, pb)
                    proj(g3, 512 + m * 128, 128, tok0, 512, pg)
                    a = f32r.next()
                    z0 = ti * 512
                    S.pool("tensor_scalar",
                        out=a[:], in0=zT[:, m, z0:z0 + 512], scalar1=pv[:, 17 + m:18 + m], scalar2=None,
                        op0=ALU.mult, r=[zT, pv], w=[a])
                    S.dve("scalar_tensor_tensor",
                        out=a[:], in0=zT[:, m, z0 + 1:z0 + 513], scalar=pv[:, 21 + m:22 + m], in1=a[:],
                        op0=ALU.mult, op1=ALU.add, r=[zT, pv, a], w=[a])
                    S.dve("scalar_tensor_tensor",
                        out=a[:], in0=zT[:, m, z0 + 2:z0 + 514], scalar=pv[:, 25 + m:26 + m], in1=a[:],
                        op0=ALU.mult, op1=ALU.add, r=[zT, pv, a], w=[a])
                    S.dve("scalar_tensor_tensor",
                        out=a[:], in0=a[:], scalar=pv[:, 29 + m:30 + m], in1=pb[:], op0=ALU.add, op1=ALU.mult,
                        r=[a, pv, pb], w=[a])
                    sg = f32r.next()
                    S.act("activation", out=sg[:], in_=pg[:], func=AF.Silu, r=[pg], w=[sg])
                    S.pool("tensor_tensor", out=oc_b[:, m, :], in0=a[:], in1=sg[:],
                                                                     op=ALU.mult, r=[a, sg], w=[oc_b])
                for m in range(8):
                    py = P.next()
                    for k in range(4):
                        S.pe("matmul", out=
                            py[:], lhsT=wbc_b[:, k, m * 128:(m + 1) * 128], rhs=oc_b[:, k, :], start=(k == 0),
                            stop=(k == 3), r=[wbc_b, oc_b], w=[py])
                    pr_ = P.next()
                    proj(g4, m * 128, 128, tok0, 512, pr_)
                    sg = f32r.next()
                    S.act("activation",
                        out=sg[:], in_=pr_[:], func=AF.Sigmoid, bias=pv[:, 33 + 8 + m:34 + 8 + m],
                        r=[pr_, pv], w=[sg])
                    yt = f32r.next()
                    S.dve("tensor_tensor", out=yt[:], in0=py[:], in1=sg[:],
                                                                         op=ALU.mult, r=[py, sg], w=[yt])
                    S.dma(
                        out=yp_o[m * 128:(m + 1) * 128, tg:tg + 512], in_=yt[:], r=[yt], w=[yp_o])

            g5 = load_group(O_QM, 512)
            load_weight_bf16(S, C, w_in, 0, 8, O_GM, 512, g5, 512, stage)
            g6 = load_group(O_RM, 1024)
            for ti in range(NT):
                tok0 = 1 + ti * 512
                tg = t_base + ti * 512
                for h in range(4):
                    pq_ = P.next()
                    proj(g5, h * 128, 128, tok0, 512, pq_)
                    sq = bf16r.next()
                    S.act("activation", out=sq[:], in_=pq_[:], func=AF.Square,
                          r=[pq_], w=[sq])
                    pss = P.next()
                    S.pe("matmul", out=pss[:], lhsT=ones[:], rhs=sq[:], start=True, stop=True,
                         r=[ones, sq], w=[pss])
                    rs = f32r.next()
                    ln_exp_rstd(pss, rs, 128, 1.0 / 128, EPS)
                    mq = bf16r.next()
                    S.dve("scalar_tensor_tensor",
                        out=mq[:], in0=pq_[:], scalar=pv[:, 57:58], in1=rs[:], op0=ALU.mult, op1=ALU.mult,
                        r=[pq_, rs, pv], w=[mq])
                    pbs = []
                    for blk in range(2):
                        psc = P.next()
                        S.pe("matmul", out=
                            psc[:], lhsT=mkT[:, h, blk * 128:(blk + 1) * 128], rhs=mq[:], start=True, stop=True,
                            r=[mkT, mq], w=[psc])
                        pb_ = bf16r.next()
                        S.act("activation", out=pb_[:], in_=psc[:], func=AF.Exp,
                                                                       scale=128 ** -0.5, r=[psc], w=[pb_])
                        pbs.append(pb_)
                    pn = P.next()
                    pd = P.next()
                    for blk in range(2):
                        S.pe("matmul", out=
                            pn[:], lhsT=mv_b[:, blk, h * 128:(h + 1) * 128], rhs=pbs[blk][:], start=(blk == 0),
                            stop=(blk == 1), r=[mv_b, pbs[blk]], w=[pn])
                    for blk in range(2):
                        S.pe("matmul", out=
                            pd[:], lhsT=ones[:], rhs=pbs[blk][:], start=(blk == 0), stop=(blk == 1),
                            r=[ones, pbs[blk]], w=[pd])
                    den = f32r.next()
                    S.dve("reciprocal", out=den[:], in_=pd[:], r=[pd], w=[den])
                    om = f32r.next()
                    S.dve("tensor_tensor", out=om[:], in0=pn[:], in1=den[:], op=ALU.mult, r=[pn, den], w=[om])
                    pg = P.next()
                    proj(g5, 512 + h * 128, 128, tok0, 512, pg)
                    sg = f32r.next()
                    S.act("activation", out=sg[:], in_=pg[:], func=AF.Silu, r=[pg], w=[sg])
                    S.pool("tensor_tensor", out=om_b[:, h, :], in0=om[:], in1=sg[:],
                                                                       op=ALU.mult, r=[om, sg], w=[om_b])
                for m in range(8):
                    py = P.next()
                    for k in range(4):
                        S.pe("matmul", out=
                            py[:], lhsT=wbm_b[:, k, m * 128:(m + 1) * 128], rhs=om_b[:, k, :], start=(k == 0),
                            stop=(k == 3), r=[wbm_b, om_b], w=[py])
                    pr_ = P.next()
                    proj(g6, m * 128, 128, tok0, 512, pr_)
                    sg = f32r.next()
                    S.act("activation",
                        out=sg[:], in_=pr_[:], func=AF.Sigmoid, bias=pv[:, 33 + 16 + m:34 + 16 + m],
                        r=[pr_, pv], w=[sg])
                    yt = f32r.next()
                    S.dve("tensor_tensor", out=yt[:], in0=py[:], in1=sg[:],
                                                                         op=ALU.mult, r=[py, sg], w=[yt])
                    S.dma(
                        out=ypm_o[m * 128:(m + 1) * 128, tg:tg + 512], in_=yt[:],
                        r=[yt], w=[ypm_o])

            g7 = load_group(O_GA, 512)
            g8 = load_group(O_RA, 1024)
            for ti in range(NT):
                tok0 = 1 + ti * 512
                tg = t_base + ti * 512
                for m in range(4):
                    pg = P.next()
                    proj(g7, m * 128, 128, tok0, 512, pg)
                    gb = bf16r.next()
                    S.act("activation", out=gb[:], in_=pg[:], func=AF.Silu, r=[pg], w=[gb])
                    S.dma(
                        out=ga_o[m * 128:(m + 1) * 128, tg:tg + 512], in_=gb[:], r=[gb], w=[ga_o])
                for m in range(8):
                    pr_ = P.next()
                    proj(g8, m * 128, 128, tok0, 512, pr_)
                    sg = f32r.next()
                    S.act("activation",
                        out=sg[:], in_=pr_[:], func=AF.Sigmoid, bias=pv[:, 33 + m:34 + m], r=[pr_, pv], w=[sg])
                    S.dma(
                        out=ra_o[m * 128:(m + 1) * 128, tg:tg + 512], in_=sg[:], r=[sg], w=[ra_o])

        S.emit(outs)
    return nc


def build_phase_b(T, SK):
    NKB = SK // 128
    QC = 512
    NQC = T // QC
    nc = bass.Bass("TRN2", target_bir_lowering=False)
    with contextlib.ExitStack() as stack:
        C = Ctx(nc, stack)
        S = Sched(nc, stack)
        qt = C.dram_in("qt", [NH * DH, T], BF16)
        kt = C.dram_in("kt", [NH * DH, SK], BF16)
        vv = C.dram_in("v", [NH * 128, NKB * DV], BF16)
        yp = C.dram_in("ypc", [D, T], F32)
        ypm = C.dram_in("ypm", [D, T], F32)
        ga = C.dram_in("ga", [512, T], BF16)
        ra = C.dram_in("ra", [D, T], F32)
        xin = C.dram_in("x", [T, D], F32)
        w_ba = C.dram_in("w_ba", [512, D], F32)
        w_out = C.dram_in("w_out", [D, D], F32)
        xo = C.dram_out("xo", [T, D], F32)
        og_d = C.dram_tmp("og_d", [512, T], BF16)

        NKC = 4 if NKB % 4 == 0 else 1
        KCW = SK // NKC
        ktb = [C.sb([96, KCW], BF16, "ktb") for _ in range(NKC)]
        vb = [C.sb([128, NKB, DV + 1], BF16, "vb") for _ in range(2)]
        qtr = C.sb_ring(3, [96, QC], BF16, "qtr")
        gab = C.sb_ring(2, [64, QC], BF16, "gab")
        ptr = C.sb_ring(4, [128, QC], BF16, "ptr")
        f32r = C.sb_ring(4, [128, 512], F32, "f32r")
        bfr = C.sb_ring(2, [128, 512], BF16, "bfr")
        stage = C.sb_ring(2, [128, 512], F32, "stage")
        sel = C.sb([DV + 1, 64], F32, "sel")
        wba_b = C.sb([128, 4, D], BF16, "wba")
        wout_b = C.sb([128, 8, D], BF16, "wout")
        ogt = C.sb_ring(1, [128, 4, 512], BF16, "ogt")
        yT = C.sb_ring(1, [128, 8, 512], BF16, "yT")
        mr = C.sb_ring(6, [128, 512], F32, "mr")
        xt_r = C.sb_ring(2, [128, D], F32, "xt")
        PS = C.ps_ring(3, [128, QC], F32, "PS")
        PO = C.ps_ring(2, [128, QC], F32, "PO")
        PM = C.ps_ring(3, [128, 512], F32, "PM")

        for i in range(2):
            S.pool("memset", ap=vb[i][:, :, DV:DV + 1], constant=1.0, w=[vb[i]])
        S.pool("memset", ap=sel[:], constant=0.0, w=[sel])
        S.pool("memset", ap=sel[DV:DV + 1, :], constant=1.0, w=[sel])

        def load_k(h, c):
            S.dma(out=ktb[c][:, :], in_=kt[h * DH:(h + 1) * DH, c * KCW:(c + 1) * KCW], r=[kt], w=[ktb[c]])

        def load_v(h):
            vb_ = vb[h % 2]
            nch = max(1, SK // 4096)
            nb = NKB // nch
            for c in range(nch):
                S.dma(out=vb_[:, c * nb:(c + 1) * nb, 0:DV],
                      in_=vv[h * 128:(h + 1) * 128, c * nb * DV:(c + 1) * nb * DV].rearrange("p (k d) -> p k d", d=DV),
                      r=[vv], w=[vb_])

        for c in range(NKC):
            load_k(0, c)
        load_v(0)
        load_weight_bf16(S, C, w_ba, 0, 4, 0, D, wba_b, 0, stage)
        load_weight_bf16(S, C, w_out, 0, 8, 0, D, wout_b, 0, stage)
        scale = DH ** -0.5
        for h in range(NH):
            if h + 1 < NH:
                load_v(h + 1)
            vb_ = vb[h % 2]
            for qc in range(NQC):
                q0 = qc * QC
                qb_ = qtr.next()
                S.dma(out=qb_[:], in_=qt[h * DH:(h + 1) * DH, q0:q0 + QC], r=[qt], w=[qb_])
                gt = gab.next()
                S.dma(out=gt[:], in_=ga[h * 64:(h + 1) * 64, q0:q0 + QC], r=[ga], w=[gt])
                po = PO.next()
                for kb in range(NKB):
                    ps_ = PS.next()
                    kb_ = ktb[(kb * 128) // KCW]
                    ko = (kb * 128) % KCW
                    S.pe("matmul", out=ps_[:], lhsT=kb_[0:96, ko:ko + 128], rhs=qb_[0:96, :],
                         start=True, stop=True, r=[kb_, qb_], w=[ps_])
                    pt = ptr.next()
                    S.act("activation", out=pt[:], in_=ps_[:], func=AF.Exp, scale=scale, r=[ps_], w=[pt])
                    S.pe("matmul", out=po[0:DV + 1, :], lhsT=vb_[:, kb, :], rhs=pt[:], start=(kb == 0),
                         stop=(kb == NKB - 1), r=[vb_, pt], w=[po])
                    if qc == NQC - 1 and h + 1 < NH and ((kb + 1) * 128) % KCW == 0:
                        load_k(h + 1, (kb * 128) // KCW)
                osb = f32r.next()
                S.dve("tensor_copy", out=osb[0:DV + 1, :], in_=po[0:DV + 1, :], r=[po], w=[osb])
                pd = PM.next()
                S.pe("matmul", out=pd[0:64, :], lhsT=sel[:, :], rhs=osb[0:DV + 1, :], start=True, stop=True,
                     r=[sel, osb], w=[pd])
                rec = f32r.next()
                S.dve("reciprocal", out=rec[0:64, :], in_=pd[0:64, :], r=[pd], w=[rec])
                S.pool("tensor_tensor", out=osb[0:64, :], in0=osb[0:64, :], in1=rec[0:64, :], op=ALU.mult,
                       r=[osb, rec], w=[osb])
                ob = bfr.next()
                S.pool("tensor_tensor", out=ob[0:64, :], in0=osb[0:64, :], in1=gt[:], op=ALU.mult,
                       r=[osb, gt], w=[ob])
                S.dma(out=og_d[h * 64:(h + 1) * 64, q0:q0 + QC], in_=ob[0:64, :], r=[ob], w=[og_d])

        for ti in range(T // 512):
            t0 = ti * 512
            og = ogt.next()
            S.dma(out=og[:], in_=og_d[:, t0:t0 + 512].rearrange("(k p) n -> p k n", p=128), r=[og_d], w=[og])
            y = yT.next()
            for m in range(8):
                rt = mr.next()
                yc = mr.next()
                ym = mr.next()
                S.dma(out=rt[:], in_=ra[m * 128:(m + 1) * 128, t0:t0 + 512], r=[ra], w=[rt])
                S.dma(out=yc[:], in_=yp[m * 128:(m + 1) * 128, t0:t0 + 512], r=[yp], w=[yc])
                S.dma(out=ym[:], in_=ypm[m * 128:(m + 1) * 128, t0:t0 + 512], r=[ypm], w=[ym])
                pm = PM.next()
                for k in range(4):
                    S.pe("matmul", out=pm[:], lhsT=wba_b[:, k, m * 128:(m + 1) * 128], rhs=og[:, k, :],
                         start=(k == 0), stop=(k == 3), r=[wba_b, og], w=[pm])
                S.dve("tensor_tensor", out=rt[:], in0=pm[:], in1=rt[:], op=ALU.mult, r=[pm, rt], w=[rt])
                S.pool("tensor_tensor", out=yc[:], in0=yc[:], in1=ym[:], op=ALU.add, r=[yc, ym], w=[yc])
                S.dve("tensor_tensor", out=y[:, m, :], in0=rt[:], in1=yc[:], op=ALU.add, r=[rt, yc], w=[y])
            for j in range(4):
                xt = xt_r.next()
                S.dma(out=xt[:], in_=xin[t0 + j * 128:t0 + (j + 1) * 128, :], r=[xin], w=[xt])
                for n in range(2):
                    pm = PM.next()
                    for k in range(8):
                        S.pe("matmul", out=pm[:], lhsT=y[:, k, j * 128:(j + 1) * 128],
                             rhs=wout_b[:, k, n * 512:(n + 1) * 512], start=(k == 0), stop=(k == 7),
                             r=[y, wout_b], w=[pm])
                    S.dve("tensor_tensor", out=xt[:, n * 512:(n + 1) * 512], in0=pm[:],
                          in1=xt[:, n * 512:(n + 1) * 512], op=ALU.add, r=[pm, xt], w=[xt])
                S.dma(out=xo[t0 + j * 128:t0 + (j + 1) * 128, :], in_=xt[:], r=[xt], w=[xo])
        S.emit([xo])
    return nc


def _pvec(L, p):
    t = np.zeros((128, NPV), np.float32)
    t[:, 0:8] = p["norm_g"][L].reshape(8, 128).T
    t[:, 8:11] = p["q_norm_g"][L].reshape(3, 128).T
    t[:, 11:13] = p["kv_norm_g"][L].reshape(2, 128).T
    qg = p["q_head_g"][L]
    kg = p["k_head_g"][L]
    t[0:96, 13] = qg
    t[64:80, 14] = qg[80:96]
    t[80:96, 14] = qg[64:80]
    t[0:96, 15] = kg
    t[64:80, 16] = kg[80:96]
    t[80:96, 16] = kg[64:80]
    cw = p["conv_w"][L]
    for tap in range(3):
        t[:, 17 + tap * 4:21 + tap * 4] = cw[tap].reshape(4, 128).T
    t[:, 29:33] = p["conv_b"][L].reshape(4, 128).T
    t[:, 33:57] = p["b_gate"][L].reshape(24, 128).T
    t[:, 57] = p["mem_q_g"][L]
    t[:, 58] = p["mem_k_g"][L]
    t[:, 59:67] = p["mem_norm_g"][L].reshape(8, 128).T
    with jax.default_device(jax.devices("cpu")[0]):
        inv_freq = np.asarray(10000.0 ** (-jnp.arange(0, 32, 2, dtype=jnp.float32) / 32), np.float32)
    t[64:80, 67] = inv_freq
    t[80:96, 67] = inv_freq
    t[64:80, 68] = -1.0
    t[80:96, 68] = 1.0
    return t


def _layer_weights(L, p):
    w_in = np.ascontiguousarray(p["w_in"][L])
    w_uq = np.ascontiguousarray(p["w_uq"][L])
    perm = np.arange(NH * DH).reshape(NH, DH).copy()
    perm[:, 64:80], perm[:, 80:96] = perm[:, 80:96].copy(), perm[:, 64:80].copy()
    w_uqr = np.ascontiguousarray(w_uq[:, perm.reshape(-1)])
    ukv = p["w_ukv"][L].reshape(KVL, NH, 128)
    w_ukv = np.ascontiguousarray(np.concatenate([ukv[:, :, :64].reshape(KVL, 512), ukv[:, :, 64:].reshape(KVL, 512)], 1))
    w_kpe = np.zeros((D, 192), np.float32)
    w_kpe[:, 64:96] = w_in[:, O_KPE:O_KPE + 32]
    w_kpe[:, 96 + 64:96 + 80] = w_in[:, O_KPE + 16:O_KPE + 32]
    w_kpe[:, 96 + 80:96 + 96] = w_in[:, O_KPE:O_KPE + 16]
    mkv = p["w_mkv"][L].reshape(D, 4, 256)
    w_mkv = np.ascontiguousarray(np.concatenate([mkv[:, :, :128].reshape(D, 512), mkv[:, :, 128:].reshape(D, 512)], 1))
    return dict(w_in=w_in, w_uq=w_uq, w_uqr=w_uqr, w_ukv=w_ukv, w_kpe=w_kpe, w_mkv=w_mkv,
                w_bc=np.ascontiguousarray(p["w_br_conv"][L]), w_bm=np.ascontiguousarray(p["w_br_mem"][L]))


_NC_CACHE = {}


def _get_nc(kind, *args):
    key = (kind,) + args
    if key not in _NC_CACHE:
        _NC_CACHE[key] = build_phase_a(*args) if kind == "a" else build_phase_b(*args)
    return _NC_CACHE[key]


def run_model(p, depth, debug=None):
    x = np.asarray(p["x"], np.float32)
    B, SEQ, _ = x.shape
    NCORE = 8
    CPB = NCORE // B
    T = SEQ // CPB
    NKB = SEQ // 128
    mem = np.asarray(p["mem"], np.float32)
    pos = np.asarray(p["positions"], np.int32)
    ident = np.eye(128, dtype=np.float32)
    p = {k: np.asarray(v) for k, v in p.items()}
    cur = x
    for L in range(depth):
        lw = _layer_weights(L, p)
        pvec = _pvec(L, p)
        in_maps = []
        for c in range(NCORE):
            b, s = divmod(c, CPB)
            xe = np.zeros((T + 128, D), np.float32)
            lo = s * T
            xe[1:T + 1] = cur[b, lo:lo + T]
            if lo > 0:
                xe[0] = cur[b, lo - 1]
            if lo + T < SEQ:
                xe[T + 1] = cur[b, lo + T]
            m = dict(xe=xe, memx=mem[b], posr=np.ascontiguousarray(np.broadcast_to(pos[b, lo:lo + T], (96, T))),
                     pvec=pvec, ident=ident)
            m.update(lw)
            in_maps.append(m)
        nca = _get_nc("a", T)
        ra = run_bass_kernel_spmd(nca, in_maps, core_ids=list(range(NCORE))).results
        if debug is not None:
            debug.append(("a", L, ra))
        in_maps = []
        for c in range(NCORE):
            b, s = divmod(c, CPB)
            grp = [b * CPB + i for i in range(CPB)]
            ktg = np.concatenate([np.asarray(ra[g]["kt"]) for g in grp], axis=1)
            vg = np.concatenate([np.asarray(ra[g]["v"]) for g in grp], axis=0)
            vg = np.ascontiguousarray(vg.reshape(NKB, 128, NH, DV).transpose(2, 1, 0, 3)).reshape(NH * 128, NKB * DV)
            lo = s * T
            in_maps.append(dict(qt=np.asarray(ra[c]["qt"]), kt=ktg, v=vg, ypc=np.asarray(ra[c]["ypc"]),
                                ypm=np.asarray(ra[c]["ypm"]),
                                ga=np.asarray(ra[c]["ga"]), ra=np.asarray(ra[c]["ra"]),
                                x=np.ascontiguousarray(cur[b, lo:lo + T]),
                                w_ba=np.ascontiguousarray(p["w_br_attn"][L]), w_out=np.ascontiguousarray(p["w_out"][L])))
        ncb = _get_nc("b", T, SEQ)
        rb = run_bass_kernel_spmd(ncb, in_maps, core_ids=list(range(NCORE))).results
        nxt = np.empty_like(cur)
        for c in range(NCORE):
            b, s = divmod(c, CPB)
            nxt[b, s * T:(s + 1) * T] = np.asarray(rb[c]["xo"])
        cur = nxt
    return cur


def kernel(**inputs):
    return run_model(inputs, 4)
```
